# Optimizing a Trainium2 kernel written in Bass

```python
import jax, jax.numpy as jnp
from jax import lax
import numpy as np

D_MODEL = 1024
BATCH = 4
SEQ = 8192
DEPTH = 1
DEC_BATCH = 32
DEC_SEQ = 64
PAST_LEN = 1024

CHUNK = 64
D_POOL = D_MODEL // 2
POOL_WINDOWS = (2, 4, 8, 16)
N_POOL_GROUPS = 4
POOL_GROUP = D_POOL // N_POOL_GROUPS
POOL_STATE = 15
D_SGU = D_MODEL // 2
N_SGU_GROUPS = 4
SGU_GROUP = D_SGU // N_SGU_GROUPS
SGU_CHUNK = 128
D_IN = 2 * D_POOL + 3 * D_SGU + 2 * D_MODEL
EPS = 1e-6

kernel_name = "pool_sgu_gated_stream_encoder"


def _rmsnorm(x, g):
    xf = x.astype(jnp.float32)
    y = xf * lax.rsqrt(jnp.mean(xf * xf, axis=-1, keepdims=True) + EPS)
    return (y * g.astype(jnp.float32)).astype(x.dtype)


def _layernorm(x, g, b):
    xf = x.astype(jnp.float32)
    mu = jnp.mean(xf, axis=-1, keepdims=True)
    var = jnp.mean(jnp.square(xf - mu), axis=-1, keepdims=True)
    y = (xf - mu) * lax.rsqrt(var + EPS)
    return (y * g.astype(jnp.float32) + b.astype(jnp.float32)).astype(x.dtype)


def _pool_mix(a, prev, pos0):
    L = a.shape[1]
    ap = jnp.concatenate([prev.astype(a.dtype), a], axis=1).astype(jnp.float32)
    c = jnp.cumsum(ap, axis=1)
    c = jnp.concatenate([jnp.zeros_like(c[:, :1]), c], axis=1)
    pos = pos0 + jnp.arange(L)
    outs = []
    for g, w in enumerate(POOL_WINDOWS):
        lo, hi = g * POOL_GROUP, (g + 1) * POOL_GROUP
        s = POOL_STATE + 1
        win = c[:, s:s + L, lo:hi] - c[:, s - w:s - w + L, lo:hi]
        cnt = jnp.minimum(pos + 1, w).astype(jnp.float32)[None, :, None]
        outs.append(win / cnt)
    mean = jnp.concatenate(outs, axis=-1)
    return (mean - a.astype(jnp.float32)).astype(a.dtype)


def _spatial_gate(u, v, w_s, b_s, chunk_len):
    B, L, _ = v.shape
    n = L // chunk_len
    blk = jnp.arange(chunk_len) // CHUNK
    mask = blk[:, None] >= blk[None, :]
    w = jnp.where(mask[None], w_s[:, :chunk_len, :chunk_len], 0.0)
    vr = v.reshape(B, n, chunk_len, N_SGU_GROUPS, SGU_GROUP)
    z = jnp.einsum('gij,bnjgc->bnigc', w, vr)
    z = z + b_s[:, :chunk_len].T[None, None, :, :, None]
    return u * z.reshape(B, L, D_SGU)


def _layer(x, prev_pool, pos0, sgu_chunk, norm_g, w_in, w_pool, pool_scale, ln_g, ln_b, w_s, b_s, w_a, w_b, w_o):
    B, L, _ = x.shape
    h = _rmsnorm(x, norm_g)
    proj = h @ w_in
    sizes = (D_POOL, D_POOL, D_SGU, D_SGU, D_SGU, D_MODEL, D_MODEL)
    idx = np.cumsum(sizes)[:-1].tolist()
    a, g_a, u, v, g_b, r_a, r_b = jnp.split(proj, idx, axis=-1)
    pa = _pool_mix(a, prev_pool, pos0).reshape(B, L, N_POOL_GROUPS, POOL_GROUP)
    pa = jnp.einsum('blgc,gcd->blgd', pa, w_pool).reshape(B, L, D_POOL) * pool_scale
    y_a = (jax.nn.silu(g_a) * pa) @ w_a
    v_hat = _layernorm(v, ln_g, ln_b)
    y_b = (jax.nn.silu(g_b) * _spatial_gate(u, v_hat, w_s, b_s, sgu_chunk)) @ w_b
    merged = jax.nn.sigmoid(r_a) * y_a + jax.nn.sigmoid(r_b) * y_b
    return x + merged @ w_o, a, v_hat


def setup_inputs(seed: int = 0) -> dict:
    key = jax.random.key(seed)
    ks = jax.random.split(key, 16)
    f32 = jnp.float32

    def nrm(k, shape, scale):
        return scale * jax.random.normal(k, shape, f32)

    return {
        "x_prompt": nrm(ks[0], (BATCH, SEQ, D_MODEL), 1.0),
        "x_sample": nrm(ks[1], (DEC_BATCH, DEC_SEQ, D_MODEL), 1.0),
        "cache_pool": nrm(ks[2], (DEPTH, DEC_BATCH, POOL_STATE, D_POOL), 1.0),
        "norm_g": 1.0 + nrm(ks[3], (DEPTH, D_MODEL), 0.05),
        "w_in": nrm(ks[4], (DEPTH, D_MODEL, D_IN), D_MODEL ** -0.5),
        "w_pool": nrm(ks[5], (DEPTH, N_POOL_GROUPS, POOL_GROUP, POOL_GROUP), POOL_GROUP ** -0.5),
        "pool_scale": 1.0 + nrm(ks[6], (DEPTH, D_POOL), 0.1),
        "ln_g": 1.0 + nrm(ks[7], (DEPTH, D_SGU), 0.05),
        "ln_b": nrm(ks[8], (DEPTH, D_SGU), 0.02),
        "w_s": nrm(ks[9], (DEPTH, N_SGU_GROUPS, SGU_CHUNK, SGU_CHUNK), SGU_CHUNK ** -0.5),
        "b_s": 1.0 + nrm(ks[10], (DEPTH, N_SGU_GROUPS, SGU_CHUNK), 0.1),
        "w_a": nrm(ks[11], (DEPTH, D_POOL, D_MODEL), D_POOL ** -0.5),
        "w_b": nrm(ks[12], (DEPTH, D_SGU, D_MODEL), D_SGU ** -0.5),
        "w_o": nrm(ks[13], (DEPTH, D_MODEL, D_MODEL), D_MODEL ** -0.5),
        "final_g": 1.0 + nrm(ks[14], (D_MODEL,), 0.05),
    }


def reference(x_prompt, x_sample, cache_pool, norm_g, w_in, w_pool, pool_scale, ln_g, ln_b, w_s, b_s,
              w_a, w_b, w_o, final_g):
    xp, xs = x_prompt, x_sample
    pool_p, pool_s, v_s = [], [], []
    for l in range(DEPTH):
        params = (norm_g[l], w_in[l], w_pool[l], pool_scale[l], ln_g[l], ln_b[l], w_s[l], b_s[l],
                  w_a[l], w_b[l], w_o[l])
        no_hist = jnp.zeros((xp.shape[0], POOL_STATE, D_POOL), xp.dtype)
        xp, a_p, _ = _layer(xp, no_hist, 0, SGU_CHUNK, *params)
        xs, a_s, vh_s = _layer(xs, cache_pool[l], PAST_LEN, xs.shape[1], *params)
        pool_p.append(a_p[:, -POOL_STATE:])
        pool_s.append(jnp.concatenate([cache_pool[l].astype(a_s.dtype), a_s], axis=1)[:, -POOL_STATE:])
        v_s.append(vh_s)
    y_prompt = _rmsnorm(xp, final_g)
    y_sample = _rmsnorm(xs, final_g)
    state_pool_prompt = jnp.stack(pool_p)
    state_pool_sample = jnp.stack(pool_s)
    state_v_sample = jnp.stack(v_s)
    return (y_prompt, y_sample, state_pool_prompt, state_pool_sample, state_v_sample)
```

```python
import numpy as np
from contextlib import ExitStack

import concourse.bass as bass
import concourse.mybir as mybir
from concourse.bass_utils import run_bass_kernel_spmd

F32 = mybir.dt.float32
BF16 = mybir.dt.bfloat16
ALU = mybir.AluOpType
AF = mybir.ActivationFunctionType

P = 128
D = 1024
DIN = 4608
DP = 512
T = 256
NSUB = 2
H = 16
EPS = 1e-6
WINDOWS = (2, 4, 8, 16)
N_PE_TR = 2
NPS = 8

C_A, C_GA, C_U, C_V, C_GB, C_RA, C_RB = 0, 512, 1024, 1536, 2048, 2560, 3584


class _Op:
    __slots__ = ("eng", "emit", "deps", "token", "is_dma", "name", "needed")


class Sched:
    ENGS = ("pe", "act", "dve", "pool", "sp")

    def __init__(self, nc, stack):
        self.nc = nc
        self.ops = {e: [] for e in self.ENGS}
        self.last_writer = {}
        self.readers = {}
        self.eng_sem = {e: stack.enter_context(nc.semaphore("prog_" + e)) for e in ("pe", "act", "dve", "pool")}
        self.stack = stack
        self.dma_sems = {}
        self.dma_cnt = {}

    def _dma_sem(self, name):
        if name not in self.dma_sems:
            self.dma_sems[name] = self.stack.enter_context(self.nc.semaphore("dma_" + name))
            self.dma_cnt[name] = 0
        return self.dma_sems[name]

    def add(self, eng, emit, reads=(), writes=(), dma=None, name=""):
        op = _Op()
        op.eng, op.emit, op.name = eng, emit, name
        op.is_dma = dma is not None
        op.needed = False
        op.token = None
        deps = []
        for k in reads:
            w = self.last_writer.get(k)
            if w is not None:
                deps.append((w, "raw"))
        for k in writes:
            w = self.last_writer.get(k)
            if w is not None:
                deps.append((w, "waw"))
            for r in self.readers.get(k, ()):
                deps.append((r, "war"))
        for k in reads:
            self.readers.setdefault(k, []).append(op)
        for k in writes:
            self.last_writer[k] = op
            self.readers[k] = []
        final = []
        seen = set()
        for d, kind in deps:
            if d is op or id(d) in seen:
                continue
            if (not op.is_dma) and (not d.is_dma) and d.eng == eng == "pe":
                continue
            seen.add(id(d))
            final.append(d)
            d.needed = True
        op.deps = final
        if op.is_dma:
            sem = self._dma_sem(dma)
            self.dma_cnt[dma] += 16
            op.token = (sem, self.dma_cnt[dma])
        self.ops[eng].append(op)
        return op

    def finalize(self):
        for e in ("pe", "act", "dve", "pool"):
            c = 0
            for op in self.ops[e]:
                if not op.is_dma and op.needed:
                    c += 1
                    op.token = (self.eng_sem[e], c)

    def emit_engine(self, eng_name, eng):
        seen = {}
        for op in self.ops[eng_name]:
            for d in op.deps:
                sem, val = d.token
                if seen.get(id(sem), 0) >= val:
                    continue
                seen[id(sem)] = val
                eng.wait_ge(sem, val)
            ins = op.emit(eng)
            if op.token is not None:
                if op.is_dma:
                    ins.then_inc(op.token[0], 16)
                else:
                    ins.then_inc(op.token[0], 1)


def build_program(NT):
    nc = bass.Bass("TRN2", target_bir_lowering=False)

    def din(name, shape):
        return nc.dram_tensor(name, list(shape), F32, kind="ExternalInput").ap()

    def dout(name, shape):
        return nc.dram_tensor(name, list(shape), F32, kind="ExternalOutput").ap()

    xp_d = din("xp", [NT * T, D])
    xh_d = din("xh", [H, D])
    xs_d = din("xs", [T, D])
    cp_d = din("cp", [60, DP])
    first_d = din("first", [P, 1])
    ident_d = din("ident", [P, P])
    ng_d = din("norm_g", [1, D])
    win_d = din("w_in", [D, DIN])
    wpool_d = din("w_pool", [4, P, P])
    psc_d = din("pool_scale", [1, DP])
    lng_d = din("ln_g", [1, DP])
    lnb_d = din("ln_b", [1, DP])
    ws_d = din("w_s", [4, P, P])
    bs_d = din("b_s", [1, DP])
    wa_d = din("w_a", [DP, D])
    wb_d = din("w_b", [DP, D])
    wo_d = din("w_o", [D, D])
    fg_d = din("final_g", [1, D])

    yp_d = dout("yp", [NT * T, D])
    ys_d = dout("ys", [T, D])
    spp_d = dout("spp", [15, DP])
    sps_d = dout("sps", [60, DP])
    sv_d = dout("sv", [T, DP])

    with ExitStack() as st:
        def sb(name, shape, dt):
            return st.enter_context(nc.sbuf_tensor(name, list(shape), dt))

        def pst(name, shape, dt):
            return st.enter_context(nc.psum_tensor(name, list(shape), dt))

        Win = sb("Win", [P, 8, DIN], BF16)
        Wa = sb("Wa", [P, 4, D], BF16)
        Wb = sb("Wb", [P, 4, D], BF16)
        Wo = sb("Wo", [P, 8, D], BF16)
        Wp = sb("Wp", [P, 4, P], BF16)
        WTp = sb("WTp", [P, 4, P], BF16)
        WTs = sb("WTs", [P, 4, P], BF16)
        g_row = sb("g_row", [P, D], F32)
        fg_row = sb("fg_row", [P, D], F32)
        B2p = sb("B2p", [P, 4, P], F32)
        B2s = sb("B2s", [P, 4, P], F32)
        identf = sb("identf", [P, P], F32)
        identb = sb("identb", [P, P], BF16)
        ones = sb("ones", [P, P], F32)
        cols = sb("cols", [P, 16], F32)
        invc = sb("invc", [P, 4, H], F32)
        dA = sb("dA", [P, 4, H], F32)
        stats = sb("stats", [P, 512], F32)
        xb = [sb("xb%d" % i, [P, NSUB, D], F32) for i in range(2)]
        xn = [sb("xn%d" % i, [P, D], BF16) for i in range(2)]
        hT = [sb("hT%d" % i, [P, 8, T], BF16) for i in range(2)]
        hTh = sb("hTh", [P, 8, H], BF16)
        a_buf = sb("a_buf", [P, 4, 320], F32)
        pA = sb("pA", [P, 4, 320], F32)
        pB = sb("pB", [P, 4, 320], F32)
        pa = sb("pa", [P, 4, T], BF16)
        sga = sb("sga", [P, 4, T], F32)
        fa = sb("fa", [P, 4, T], BF16)
        sra = sb("sra", [P, 8, T], F32)
        srb = sb("srb", [P, 8, T], F32)
        mg = sb("mg", [P, 8, T], BF16)
        n_buf = sb("n_buf", [P, NSUB, DP], BF16)
        sgb = sb("sgb", [P, 4, T], F32)
        s_bf = sb("s_bf", [P, 4, T], BF16)
        tmpc = sb("tmpc", [P, H], F32)
        cptb = sb("cptb", [64, 512], F32)

        ps = [pst("ps%d" % i, [P, 512], F32) for i in range(NPS)]

        S = Sched(nc, st)

        def v4(ap2):
            return ap2.rearrange("p a (b c) -> p (a b) c", c=P)
        Wnat_p, kWnp = v4(srb[:, 0:2, :]), ("srb", 0)
        Wnat_s, kWns = v4(srb[:, 2:4, :]), ("srb", 1)
        WT32p, kW32p = v4(srb[:, 4:6, :]), ("srb", 2)
        WT32s, kW32s = v4(srb[:, 6:8, :]), ("srb", 3)
        mgf = mg[:, :, :].rearrange("p a b -> p (a b)").bitcast(F32)
        bsr_p = mgf[:, 0:512].rearrange("p (g i) -> p g i", g=4)
        bsr_s = mgf[:, 512:1024].rearrange("p (g i) -> p g i", g=4)
        srb_flat = srb[:, :, :].rearrange("p a b -> p (a b)")
        sra_flat = sra[:, :, :].rearrange("p a b -> p (a b)")
        stp = sra_flat[0:H, 0:512]
        sga_flat = sga[:, :, :].rearrange("p a b -> p (a b)")
        pA_flat = pA[:, :, :].rearrange("p a b -> p (a b)")
        pB_flat = pB[:, :, :].rearrange("p a b -> p (a b)")
        n32 = pA_flat[:, 0:1024].rearrange("p (s c) -> p s c", s=NSUB)
        lng_row = pB_flat[:, 0:512]
        lnb_row = pB_flat[:, 512:1024]

        lng_col = lambda g: cols[:, g:g + 1]
        lnb_col = lambda g: cols[:, 4 + g:5 + g]
        psc_col = lambda g: cols[:, 8 + g:9 + g]
        first_col = cols[:, 12:13]
        neghalf = cols[:, 13:14]

        stat_ctr = [0]

        def stat(n=1):
            slot = stat_ctr[0] % 64
            stat_ctr[0] += 1
            return stats[:, slot * 8:slot * 8 + n], ("stat", slot)

        ps_ctr = [0]

        def next_ps():
            k = ps_ctr[0] % NPS
            ps_ctr[0] += 1
            return ps[k], ("ps", k)

        uniq = [0]

        def dma(eng, out, in_, reads, writes, sem, name=""):
            if sem[0] in "cw":
                uniq[0] += 1
                sem = "%s_%d" % (sem, uniq[0])
            return S.add(eng, lambda e, o=out, i=in_: e.dma_start(out=o, in_=i), reads=reads, writes=writes,
                         dma=sem, name=name)

        def x_rows(src, r0):
            return src[r0:r0 + T, :].rearrange("(s p) d -> p s d", p=P)

        tiles = [("p", i) for i in range(NT)] + [("s", 0)]
        NTT = len(tiles)

        def load_x(ti):
            kind, i = tiles[ti]
            b = ti % 2
            src = x_rows(xp_d, i * T) if kind == "p" else x_rows(xs_d, 0)
            return dma("sp", xb[b][:, :, :], src, [], [("xb", b, 0), ("xb", b, 1)], "x%d" % b, "loadx")

        cpt = cptb[:, :]
        S.add("dve", lambda e: e.memset(cpt, 0.0), [], ["cpt_m"])
        dma("sp", xb[1][0:H, 0, :], xh_d[:, :], [], [("xb", 1, 0)], "x1", "loadxh")
        dma("sp", g_row[:, :], ng_d.partition_broadcast(P), [], ["g_row"], "c")
        load_x(0)
        dma("sp", identf[:, :], ident_d[:, :], [], ["identf"], "c")
        dma("pool", identb[:, :], ident_d[:, :], [], ["identb"], "c")
        S.add("dve", lambda e: e.memset(cols[:, 13:14], -0.5), [], [("cols", 13)])
        S.add("dve", lambda e: e.memset(ones[:, :], 1.0), [], ["ones"])

        win_v = win_d.rearrange("(k p) n -> p k n", p=P)

        def load_win(cb0, cb1):
            dma("pool", Win[:, :, cb0 * 512:cb1 * 512], win_v[:, :, cb0 * 512:cb1 * 512], [],
                [("Win", cb) for cb in range(cb0, cb1)], "w", "win")
        load_win(0, 2)
        dma("pool", Wp[:, :, :], wpool_d.rearrange("g c d -> c g d"), [], ["Wp"], "w", "wp")
        load_win(2, 5)
        load_win(5, 7)
        S.add("dve", lambda e: e.memset(Wnat_s, 0.0), [], [kWns])
        ws_v = ws_d.rearrange("g i j -> i g j")
        dma("pool", Wnat_p, ws_v, [], [kWnp], "w")
        dma("pool", Wnat_s[64:128, :, 64:128], ws_v[0:64, :, 0:64], [], [kWns], "w")
        dma("pool", mgf[:, 0:512], bs_d.partition_broadcast(P), [], ["bsr_p"], "w")
        load_win(7, 9)
        dma("pool", Wa[:, :, :], wa_d.rearrange("(k p) n -> p k n", p=P), [], ["Wa"], "w", "wa")
        dma("pool", Wb[:, :, :], wb_d.rearrange("(k p) n -> p k n", p=P), [], ["Wb"], "w", "wb")
        dma("pool", Wo[:, :, :], wo_d.rearrange("(k p) n -> p k n", p=P), [], ["Wo"], "w", "wo")
        dma("pool", fg_row[:, :], fg_d.partition_broadcast(P), [], ["fg_row"], "w", "fg")

        def rsqrt_chain(src, pre_scale, reads, nparts=P):
            tt, ktt = stat()
            rs, krs = stat()
            S.add("dve", lambda e: e.tensor_scalar(out=tt[0:nparts, :], in0=src, scalar1=pre_scale, scalar2=EPS,
                                                   op0=ALU.mult, op1=ALU.add), reads, [ktt])
            S.add("pool", lambda e: e.tensor_tensor(out=rs[0:nparts, :], in0=tt[0:nparts, :],
                                                    in1=neghalf[0:nparts, :], op=ALU.pow),
                  [ktt, ("cols", 13)], [krs])
            return rs, krs

        def xprep_sub(xsrc, kx, s_idx, nparts=P):
            ss, kss = stat()
            kxn = ("xn", s_idx)
            S.add("act", lambda e: e.activation(out=xn[s_idx][0:nparts, :], in_=xsrc, func=AF.Square,
                                                accum_out=ss[0:nparts, :]), [kx], [kss, kxn])
            rs, krs = rsqrt_chain(ss[0:nparts, :], 1.0 / D, [kss], nparts)
            S.add("dve", lambda e: e.scalar_tensor_tensor(out=xn[s_idx][0:nparts, :], in0=xsrc,
                                                          scalar=rs[0:nparts, :], in1=g_row[0:nparts, :],
                                                          op0=ALU.mult, op1=ALU.mult),
                  [kx, krs, "g_row"], [kxn])

        def xprep_tile(ti):
            b = ti % 2
            for s in range(NSUB):
                xprep_sub(xb[b][:, s, :], ("xb", b, s), s)

        def bfview(bank):
            return bank[:, :].bitcast(BF16).rearrange("p (k t) -> p k t", k=8)

        def transposes_tile_pe(ti):
            b = ti % 2
            for s in range(NSUB):
                bank, bk = next_ps()
                ptv = bfview(bank)

                def tr(e, s=s, ptv=ptv):
                    ins = None
                    for kc in range(8):
                        ins = e.transpose(out=ptv[:, kc, :], in_=xn[s][:, kc * P:(kc + 1) * P],
                                          identity=identb[:, :])
                    return ins
                S.add("pe", tr, [("xn", s), "identb"], [bk])
                S.add("act", lambda e, s=s, b=b, ptv=ptv: e.activation(out=hT[b][:, :, s * P:(s + 1) * P],
                                                                        in_=ptv, func=AF.Copy),
                      [bk], [("hT", b, s)])

        def transposes_tile(ti):
            b = ti % 2
            for s in range(NSUB):
                S.add("sp", lambda e, s=s, b=b: e.dma_start_transpose(out=hT[b][:, :, s * P:(s + 1) * P],
                                                                      in_=xn[s][:, :]),
                      reads=[("xn", s)], writes=[("hT", b, s)], dma="tr%d%d" % (b, s), name="xpose")

        def proj_pb(col0, rhs_of_kc, nk, wkeys, lhs_of, rkeys):
            bank, bk = next_ps()

            def mm(e):
                ins = None
                for h in range(2):
                    for kc in range(nk):
                        ins = e.matmul(out=bank[:, h * T:(h + 1) * T], lhsT=lhs_of(kc, col0 + h * P),
                                       rhs=rhs_of_kc(kc), start=(kc == 0), stop=(kc == nk - 1))
                return ins
            S.add("pe", mm, list(wkeys) + list(rkeys), [bk])
            return bank, bk

        def win_pb(b, col0):
            return proj_pb(col0, lambda kc: hT[b][:, kc, :], 8, [("Win", col0 // 512)],
                           lambda kc, c: Win[:, kc, c:c + P], [("hT", b, 0), ("hT", b, 1)])

        def bank3(bank):
            return bank[:, :].rearrange("p (h t) -> p h t", h=2)

        def pv(buf, g0, g1, lo, hi, sample):
            if sample:
                return buf[:, g0:g1, :].rearrange("p g (s l) -> p g s l", s=4)[:, :, :, lo:hi]
            return buf[:, g0:g1, lo:hi]

        def pv1(buf, g, lo, hi, sample):
            if sample:
                return buf[:, g, :].rearrange("p (s l) -> p s l", s=4)[:, :, lo:hi]
            return buf[:, g, lo:hi]

        def pooling(sample, first_tile, carry=True):
            L = 80 if sample else H + T
            A, B, X = pA, pB, a_buf

            def add(dst, src, g0, g1, lo, k, rk, wk):
                S.add("dve", lambda e: e.tensor_tensor(out=pv(dst, g0, g1, lo, L, sample),
                                                       in0=pv(src, g0, g1, lo, L, sample),
                                                       in1=pv(src, g0, g1, lo - k, L - k, sample), op=ALU.add),
                      rk, wk)
            add(A, X, 0, 4, 1, 1, ["a_buf"], ["pA"])
            add(B, A, 1, 4, 3, 2, ["pA"], ["pB"])
            add(A, B, 2, 4, 7, 4, ["pB"], ["pA"])
            add(B, A, 3, 4, 15, 8, ["pA"], ["pB"])
            srcs = (A, B, A, B)
            for g, w in enumerate(WINDOWS):
                Sb = srcs[g]
                outv = pa[:, g, :].rearrange("p (s l) -> p s l", s=4) if sample else pa[:, g, :]
                S.add("dve", lambda e, Sb=Sb, g=g, w=w, outv=outv: e.scalar_tensor_tensor(
                    out=outv, in0=pv1(Sb, g, H, L, sample), scalar=1.0 / w, in1=pv1(X, g, H, L, sample),
                    op0=ALU.mult, op1=ALU.subtract), ["pA", "pB", "a_buf"], [("pa", g)])
                if first_tile:
                    S.add("dve", lambda e, Sb=Sb, g=g: e.tensor_tensor(
                        out=tmpc[:, :], in0=Sb[:, g, H:2 * H], in1=invc[:, g, :], op=ALU.mult),
                        ["pA", "pB", "invc"], ["tmpc"])
                    S.add("dve", lambda e, g=g: e.tensor_tensor(
                        out=pa[:, g, 0:H], in0=tmpc[:, :], in1=X[:, g, H:2 * H], op=ALU.subtract),
                        ["tmpc", "a_buf"], [("pa", g)])
            if (not sample) and carry:
                S.add("dve", lambda e: e.tensor_copy(out=X[:, :, 0:H], in_=X[:, :, T:T + H]),
                      ["a_buf", "pA", "pB"], ["a_buf"])

        st_ctr = [0]

        def state_out(src_of_g, dst_rows, sem):
            bank, bk = next_ps()

            def tr(e):
                ins = None
                for g in range(4):
                    ins = e.transpose(out=bank[0:H, g * P:(g + 1) * P], in_=src_of_g(g), identity=identf[:, :])
                return ins
            S.add("pe", tr, ["a_buf", "identf"], [bk])
            k = st_ctr[0] % 4
            st_ctr[0] += 1
            S.add("act", lambda e: e.activation(out=srb_flat[0:H, 512 * k:512 * (k + 1)], in_=bank[0:H, :],
                                                func=AF.Copy), [bk], [("srb", k)])
            return dma("sp", dst_rows, srb_flat[1:H, 512 * k:512 * (k + 1)], [("srb", k)], [], "%s%d" % (sem, k),
                       "state")

        out_ops = []

        xprep_sub(xb[1][0:H, 0, :], ("xb", 1, 0), 1, nparts=H)

        bankt, bkt = next_ps()
        ptvh = bfview(bankt)

        def trh(e):
            ins = None
            for kc in range(8):
                ins = e.transpose(out=ptvh[:, kc, 0:H], in_=xn[1][0:H, kc * P:(kc + 1) * P],
                                  identity=identb[0:H, 0:H])
            return ins
        S.add("pe", trh, [("xn", 1), "identb"], [bkt])
        S.add("act", lambda e: e.activation(out=hTh[:, :, :], in_=ptvh[:, :, 0:H], func=AF.Copy),
              [bkt], ["hTh"])
        xprep_tile(0)
        transposes_tile_pe(0)
        bankh, bkh = next_ps()

        def mmh(e):
            ins = None
            for oc in range(4):
                for kc in range(8):
                    ins = e.matmul(out=bankh[:, oc * H:(oc + 1) * H], lhsT=Win[:, kc, oc * P:(oc + 1) * P],
                                   rhs=hTh[:, kc, :], start=(kc == 0), stop=(kc == 7))
            return ins
        S.add("pe", mmh, [("Win", 0), "hTh"], [bkh])
        S.add("act", lambda e: e.activation(out=a_buf[:, :, 0:H],
                                            in_=bankh[:, 0:4 * H].rearrange("p (g h) -> p g h", g=4),
                                            func=AF.Copy), [bkh], ["a_buf"])

        rows12 = sra_flat[0:12, 512:640]
        krows = [("sra", 1), "rows_a", "rows_b", "rows_c"]
        for vi, (vec_d, kk) in enumerate(((lng_d, "rows_a"), (lnb_d, "rows_b"), (psc_d, "rows_c"))):
            dma("sp", rows12[4 * vi:4 * vi + 4, :], vec_d[0:1, :].rearrange("o (g p) -> (o g) p", g=4),
                [], [kk], "c")
        dma("sp", cols[:, 12:13], first_d[:, :], [], [("cols", 12)], "c")
        def make_cols():
            bankr, bkr = next_ps()

            def mmc(e):
                return e.transpose(out=bankr[:, 0:12], in_=rows12, identity=identf[0:12, 0:12])
            S.add("pe", mmc, ["identf"] + krows, [bkr])
            S.add("act", lambda e: e.activation(out=cols[:, 0:12], in_=bankr[:, 0:12], func=AF.Copy),
                  [bkr], [("cols", j) for j in range(12)])

        def late_consts(ti):
            if ti == min(2, NTT - 2):
                for sg in range(4):
                    dma("sp", cptb[sg * H + 1:(sg + 1) * H, :], cp_d[sg * 15:(sg + 1) * 15, :], ["cpt_m"],
                        [("cpt", sg)], "c", "cp")

        kbss_all = ["bsr_s"]

        def setup_spatial_pre():
            S.add("dve", lambda e: e.tensor_copy(out=Wnat_s[0:64, :, 0:64], in_=Wnat_p[0:64, :, 0:64]),
                  [kWnp], [kWns])
            S.add("dve", lambda e: e.memset(Wnat_p[0:64, :, 64:128], 0.0), [], [kWnp])
            for hh in range(2):
                S.add("dve", lambda e, hh=hh: e.tensor_copy(out=bsr_s[:, :, hh * 64:(hh + 1) * 64],
                                                             in_=bsr_p[:, :, 0:64]), ["bsr_p"], ["bsr_s"])

        def setup_spatial():
          setup_spatial_pre()
          modes = (("p", Wnat_p, kWnp, WT32p, kW32p, WTp, bsr_p, ["bsr_p"], B2p),
                   ("s", Wnat_s, kWns, WT32s, kW32s, WTs, bsr_s, kbss_all, B2s))
          for (tag, Wnat, kWn, WT32, kW32, WTb, bsr, kbs, B2) in modes:
            bank, bk = next_ps()

            def tr(e, Wnat=Wnat, bank=bank):
                ins = None
                for g in range(4):
                    ins = e.transpose(out=bank[:, g * P:(g + 1) * P], in_=Wnat[:, g, :], identity=identf[:, :])
                return ins
            S.add("pe", tr, [kWn, "identf"], [bk])
            S.add("act", lambda e, WT32=WT32, bank=bank: e.activation(
                out=WT32, in_=bank[:, :].rearrange("p (g i) -> p g i", g=4), func=AF.Copy), [bk], [kW32])
            S.add("dve", lambda e, WTb=WTb, WT32=WT32: e.tensor_copy(out=WTb[:, :, :], in_=WT32),
                  [kW32], [("WT", tag)])
          for (tag, Wnat, kWn, WT32, kW32, WTb, bsr, kbs, B2) in modes:
            bank2, bk2 = next_ps()

            def rw(e, WT32=WT32, bank2=bank2):
                ins = None
                for g in range(4):
                    ins = e.matmul(out=bank2[:, g * P:(g + 1) * P], lhsT=ones[:, :], rhs=WT32[:, g, :],
                                   start=True, stop=True)
                return ins
            S.add("pe", rw, [kW32, "ones"], [bk2])
            for g in range(4):
                S.add("dve", lambda e, g=g, B2=B2, bank2=bank2, bsr=bsr: e.scalar_tensor_tensor(
                    out=B2[:, g, :], in0=bank2[:, g * P:(g + 1) * P], scalar=lnb_col(g), in1=bsr[:, g, :],
                    op0=ALU.mult, op1=ALU.add), [bk2, ("cols", 4 + g)] + [("mg", q) for q in range(4)] + list(kbs),
                    [("B2", tag)])

        S.add("dve", lambda e: e.memset(dA[:, :, :], 0.0), [], ["dA"])
        for g, w in enumerate(WINDOWS):
            for t in range(w - 1):
                S.add("dve", lambda e, g=g, t=t, w=w: e.memset(dA[:, g, t:t + 1], 1.0 / (t + 1) - 1.0 / w),
                      [], ["dA"])
        for g, w in enumerate(WINDOWS):
            S.add("dve", lambda e, g=g, w=w: e.tensor_scalar(
                out=invc[:, g, :], in0=dA[:, g, :], scalar1=first_col, scalar2=1.0 / w,
                op0=ALU.mult, op1=ALU.add), ["dA", ("cols", 12)], ["invc"])

        def stage_A(ti):
            kind, i = tiles[ti]
            sample = kind == "s"
            b = ti % 2
            tag = "s" if sample else "p"
            B2 = B2s if sample else B2p
            WT = WTs if sample else WTp

            if sample:
                bankc, bkc = next_ps()

                def trc(e, bankc=bankc):
                    ins = None
                    for g in range(4):
                        ins = e.transpose(out=bankc[:, g * 64:(g + 1) * 64], in_=cpt[:, g * P:(g + 1) * P],
                                          identity=identf[0:64, 0:64])
                    return ins
                S.add("pe", trc, ["cpt_m"] + [("cpt", sg) for sg in range(4)] + ["identf"], [bkc])
                S.add("act", lambda e, bankc=bankc: e.activation(
                    out=a_buf[:, :, :].rearrange("p g (s l) -> p g s l", s=4)[:, :, :, 0:H],
                    in_=bankc[:, 0:256].rearrange("p (g s h) -> p g s h", g=4, s=4), func=AF.Copy),
                    [bkc], ["a_buf"])

            for pb in range(2):
                bank, bk = win_pb(b, C_A + pb * 256)
                if sample:
                    outv = a_buf[:, 2 * pb:2 * pb + 2, :].rearrange("p g (s l) -> p g s l", s=4)[:, :, :, H:80]
                    inv = bank[:, :].rearrange("p (g s l) -> p g s l", g=2, s=4)
                else:
                    outv = a_buf[:, 2 * pb:2 * pb + 2, H:H + T]
                    inv = bank3(bank)
                S.add("act", lambda e, outv=outv, inv=inv: e.activation(out=outv, in_=inv, func=AF.Copy),
                      [bk], ["a_buf"])
            for pb in range(2):
                bank, bk = win_pb(b, C_GA + pb * 256)
                S.add("act", lambda e, pb=pb, bank=bank: e.activation(
                    out=sga[:, 2 * pb:2 * pb + 2, :], in_=bank3(bank), func=AF.Silu), [bk], [("sga", pb)])
            pooling(sample, first_tile=(ti == 0), carry=not ((not sample) and i == NT - 1))
            if sample:
                dma("sp", lng_row, lng_d.partition_broadcast(P), [], ["pB"], "c")
                dma("sp", lnb_row, lnb_d.partition_broadcast(P), [], ["pB"], "c")

        def state_outs(ti):
            kind, i = tiles[ti]
            if kind == "p" and i == NT - 1:
                out_ops.append(state_out(lambda g: a_buf[:, g, T:T + H], spp_d[:, :], "st"))
            if kind == "s":
                for sg in range(4):
                    out_ops.append(state_out(
                        lambda g, sg=sg: a_buf[:, g, sg * 80 + 64:sg * 80 + 80],
                        sps_d[sg * 15:(sg + 1) * 15, :], "st"))


        def stage_B(ti):
            kind, i = tiles[ti]
            sample = kind == "s"
            b = ti % 2
            tag = "s" if sample else "p"
            B2 = B2s if sample else B2p
            WT = WTs if sample else WTp

            if ti + 1 < NTT:
                load_x(ti + 1)
            late_consts(ti)
            if ti == 0:
                make_cols()

            ln_defer = []
            for s in range(NSUB):
                bank, bk = next_ps()

                def mmv(e, s=s, bank=bank):
                    ins = None
                    for kc in range(8):
                        ins = e.matmul(out=bank[:, :], lhsT=hT[b][:, kc, s * P:(s + 1) * P],
                                       rhs=Win[:, kc, C_V:C_V + 512], start=(kc == 0), stop=(kc == 7))
                    return ins
                S.add("pe", mmv, [("Win", 3), ("hT", b, s)], [bk])
                st6, k6 = stat(6)
                mv, kmv = stat(2)
                nmr, knm = stat()
                S.add("dve", lambda e, st6=st6, bank=bank: e.bn_stats(out=st6, in_=bank[:, :]), [bk], [k6])
                S.add("dve", lambda e, st6=st6, mv=mv: e.bn_aggr(out=mv, in_=st6), [k6], [kmv])
                rstd, krs = rsqrt_chain(mv[:, 1:2], 1.0, [kmv])
                S.add("dve", lambda e, mv=mv, rstd=rstd, nmr=nmr: e.tensor_scalar(
                    out=nmr, in0=mv[:, 0:1], scalar1=rstd, scalar2=-1.0, op0=ALU.mult, op1=ALU.mult),
                    [kmv, krs], [knm])
                ln_defer.append((s, bank, bk, rstd, krs, nmr, knm))

            def ln_finish():
              for (s, bank, bk, rstd, krs, nmr, knm) in ln_defer:
                S.add("act", lambda e, s=s, bank=bank, rstd=rstd, nmr=nmr: e.activation(
                    out=n_buf[:, s, :], in_=bank[:, :], func=AF.Identity, scale=rstd, bias=nmr),
                    [bk, krs, knm], [("n", s)])
                if sample:
                    S.add("act", lambda e, s=s, bank=bank, rstd=rstd, nmr=nmr: e.activation(
                        out=n32[:, s, :], in_=bank[:, :], func=AF.Identity, scale=rstd, bias=nmr),
                        [bk, krs, knm], ["pA"])
                    S.add("dve", lambda e, s=s: e.tensor_tensor(out=n32[:, s, :], in0=n32[:, s, :], in1=lng_row,
                                                                op=ALU.mult), ["pA", "pB"], ["pA"])
                    S.add("dve", lambda e, s=s: e.tensor_tensor(out=n32[:, s, :], in0=n32[:, s, :], in1=lnb_row,
                                                                op=ALU.add), ["pA", "pB"], ["pA"])
                    out_ops.append(dma("sp", sv_d[s * P:(s + 1) * P, :], n32[:, s, :], ["pA"], [], "sv%d" % s))

            for pb in range(2):
                bank, bk = win_pb(b, C_GB + pb * 256)
                S.add("act", lambda e, pb=pb, bank=bank: e.activation(
                    out=sgb[:, 2 * pb:2 * pb + 2, :], in_=bank3(bank), func=AF.Silu), [bk], [("sgb", pb)])
            for pb in range(2):
                bank, bk = win_pb(b, C_U + pb * 256)
                S.add("dve", lambda e, pb=pb, bank=bank: e.tensor_tensor(
                    out=sgb[:, 2 * pb:2 * pb + 2, :], in0=bank3(bank), in1=sgb[:, 2 * pb:2 * pb + 2, :],
                    op=ALU.mult), [bk, ("sgb", pb)], [("sgb", pb)])

            ln_finish()
            if ti + 1 < NTT:
                xprep_tile(ti + 1)
                if ti + 1 >= N_PE_TR:
                    transposes_tile(ti + 1)

            for pb in range(4):
                bank, bk = win_pb(b, C_RA + pb * 256)
                S.add("act", lambda e, pb=pb, bank=bank: e.activation(
                    out=sra[:, 2 * pb:2 * pb + 2, :], in_=bank3(bank), func=AF.Sigmoid),
                    [bk], [("sra", pb)])

            if ti == 0:
                setup_spatial()
            state_outs(ti)

            for pb in range(2):
                bank, bk = next_ps()

                def mmp(e, pb=pb, bank=bank):
                    ins = None
                    for h in range(2):
                        g = 2 * pb + h
                        ins = e.matmul(out=bank[:, h * T:(h + 1) * T], lhsT=Wp[:, g, :], rhs=pa[:, g, :],
                                       start=True, stop=True)
                    return ins
                S.add("pe", mmp, ["Wp", ("pa", 2 * pb), ("pa", 2 * pb + 1)], [bk])
                for h in range(2):
                    g = 2 * pb + h
                    S.add("dve", lambda e, g=g, h=h, bank=bank: e.scalar_tensor_tensor(
                        out=fa[:, g, :], in0=bank[:, h * T:(h + 1) * T], scalar=psc_col(g), in1=sga[:, g, :],
                        op0=ALU.mult, op1=ALU.mult), [bk, ("cols", 8 + g), ("sga", pb)], [("fa", g)])

            for pb in range(2):
                bank, bk = next_ps()

                def mms(e, pb=pb, bank=bank):
                    ins = None
                    for h in range(2):
                        g = 2 * pb + h
                        for s in range(NSUB):
                            ins = e.matmul(out=bank[:, h * T + s * P:h * T + (s + 1) * P],
                                           lhsT=n_buf[:, s, g * P:(g + 1) * P], rhs=WT[:, g, :],
                                           start=True, stop=True)
                    return ins
                S.add("pe", mms, [("WT", tag), ("n", 0), ("n", 1)], [bk])
                for h in range(2):
                    g = 2 * pb + h
                    for s in range(NSUB):
                        S.add("dve", lambda e, g=g, h=h, s=s, bank=bank: e.scalar_tensor_tensor(
                            out=sga[:, g, s * P:(s + 1) * P], in0=bank[:, h * T + s * P:h * T + (s + 1) * P],
                            scalar=lng_col(g), in1=B2[:, g, :], op0=ALU.mult, op1=ALU.add),
                            [bk, ("cols", g), ("B2", tag)], [("sga", pb)])
                S.add("dve", lambda e, pb=pb: e.tensor_tensor(
                    out=s_bf[:, 2 * pb:2 * pb + 2, :], in0=sga[:, 2 * pb:2 * pb + 2, :],
                    in1=sgb[:, 2 * pb:2 * pb + 2, :], op=ALU.mult), [("sga", pb), ("sgb", pb)], [("s", pb)])

            for pb in range(4):
                bank, bk = win_pb(b, C_RB + pb * 256)
                S.add("act", lambda e, pb=pb, bank=bank: e.activation(
                    out=srb[:, 2 * pb:2 * pb + 2, :], in_=bank3(bank), func=AF.Sigmoid),
                    [bk], [("srb", pb)])

            for pb in range(4):
                bank, bk = proj_pb(pb * 256, lambda kc: fa[:, kc, :], 4, ["Wa"],
                                   lambda kc, c: Wa[:, kc, c:c + P], [("fa", g) for g in range(4)])
                S.add("dve", lambda e, pb=pb, bank=bank: e.tensor_tensor(
                    out=sra[:, 2 * pb:2 * pb + 2, :], in0=bank3(bank), in1=sra[:, 2 * pb:2 * pb + 2, :],
                    op=ALU.mult), [bk, ("sra", pb)], [("sra", pb)])

            if ti + 1 < NTT and ti + 1 < N_PE_TR:
                transposes_tile_pe(ti + 1)

            for pb in range(4):
                bank, bk = proj_pb(pb * 256, lambda kc: s_bf[:, kc, :], 4, ["Wb"],
                                   lambda kc, c: Wb[:, kc, c:c + P], [("s", 0), ("s", 1)])
                S.add("dve", lambda e, pb=pb, bank=bank: e.tensor_tensor(
                    out=srb[:, 2 * pb:2 * pb + 2, :], in0=bank3(bank), in1=srb[:, 2 * pb:2 * pb + 2, :],
                    op=ALU.mult), [bk, ("srb", pb)], [("srb", pb)])
                if pb >= 1:
                    q = pb - 1
                    S.add("dve", lambda e, q=q: e.tensor_tensor(
                        out=mg[:, 2 * q:2 * q + 2, :], in0=sra[:, 2 * q:2 * q + 2, :],
                        in1=srb[:, 2 * q:2 * q + 2, :], op=ALU.add), [("sra", q), ("srb", q)], [("mg", q)])
            S.add("dve", lambda e: e.tensor_tensor(out=mg[:, 6:8, :], in0=sra[:, 6:8, :], in1=srb[:, 6:8, :],
                                                   op=ALU.add), [("sra", 3), ("srb", 3)], [("mg", 3)])


        def stage_C(ti):
            kind, i = tiles[ti]
            sample = kind == "s"
            b = ti % 2
            tag = "s" if sample else "p"
            B2 = B2s if sample else B2p
            WT = WTs if sample else WTp

            ydst = yp_d if not sample else ys_d
            r0 = i * T if not sample else 0
            last = ti == NTT - 1

            def wo_group(s, h):
                bank, bk = next_ps()
                if not last:
                    def mmo(e):
                        ins = None
                        for kc in range(8):
                            ins = e.matmul(out=bank[:, :], lhsT=mg[:, kc, s * P:(s + 1) * P],
                                           rhs=Wo[:, kc, h * 512:(h + 1) * 512], start=(kc == 0), stop=(kc == 7))
                        return ins
                    S.add("pe", mmo, ["Wo"] + [("mg", q) for q in range(4)], [bk])
                return bank, bk

            def resid(s, h, bank, bk):
                S.add("dve", lambda e: e.tensor_tensor(
                    out=xb[b][:, s, h * 512:(h + 1) * 512], in0=bank[:, :],
                    in1=xb[b][:, s, h * 512:(h + 1) * 512], op=ALU.add), [bk, ("xb", b, s)], [("xb", b, s)])

            def final_norm(s):
                ss2, kss2 = stat()
                S.add("act", lambda e: e.activation(out=xn[s][:, :], in_=xb[b][:, s, :], func=AF.Square,
                                                    accum_out=ss2), [("xb", b, s)], [kss2, ("xn", s)])
                rs2, krs2 = rsqrt_chain(ss2, 1.0 / D, [kss2])
                S.add("dve", lambda e: e.scalar_tensor_tensor(
                    out=xb[b][:, s, :], in0=xb[b][:, s, :], scalar=rs2, in1=fg_row[:, :],
                    op0=ALU.mult, op1=ALU.mult), [("xb", b, s), krs2, "fg_row"], [("xb", b, s)])
                out_ops.append(dma("sp", ydst[r0 + s * P:r0 + (s + 1) * P, :], xb[b][:, s, :],
                                   [("xb", b, s)], [], "y%d%d" % (b, s), "store"))

            if not last:
                for s in range(NSUB):
                    for h in range(2):
                        bank, bk = wo_group(s, h)
                        resid(s, h, bank, bk)
                for s in range(NSUB):
                    final_norm(s)
            else:
                groups = [(s, h) + wo_group(s, h) for s in range(NSUB) for h in range(2)]
                for q in range(4):
                    for (s, h, bank, bk) in groups:
                        def mmq(e, s=s, h=h, bank=bank, q=q):
                            ins = None
                            for kc in (2 * q, 2 * q + 1):
                                ins = e.matmul(out=bank[:, :], lhsT=mg[:, kc, s * P:(s + 1) * P],
                                               rhs=Wo[:, kc, h * 512:(h + 1) * 512], start=(kc == 0),
                                               stop=(kc == 7))
                            return ins
                        S.add("pe", mmq, ["Wo", ("mg", q)], [bk])
                for (s, h, bank, bk) in groups:
                    resid(s, h, bank, bk)
                for s in range(NSUB):
                    final_norm(s)


        stage_A(0)
        for ti in range(NTT):
            stage_B(ti)
            if ti + 1 < NTT:
                stage_A(ti + 1)
            stage_C(ti)

        fin = _Op()
        fin.eng, fin.name, fin.is_dma, fin.needed, fin.token = "sp", "final", False, False, None
        fin.deps = list(out_ops)
        fin.emit = lambda e: e.nop()
        S.ops["sp"].append(fin)

        S.finalize()
        with nc.Block() as block:
            @block.sync
            def _(eng):
                S.emit_engine("sp", eng)

            @block.tensor
            def _(eng):
                S.emit_engine("pe", eng)

            @block.scalar
            def _(eng):
                S.emit_engine("act", eng)

            @block.vector
            def _(eng):
                S.emit_engine("dve", eng)

            @block.gpsimd
            def _(eng):
                S.emit_engine("pool", eng)
    return nc


_PROG_CACHE = {}


def kernel(x_prompt, x_sample, cache_pool, norm_g, w_in, w_pool, pool_scale, ln_g, ln_b, w_s, b_s,
           w_a, w_b, w_o, final_g):
    f = lambda a: np.ascontiguousarray(np.asarray(a, dtype=np.float32))
    x_prompt, x_sample, cache_pool = f(x_prompt), f(x_sample), f(cache_pool)
    B, SEQ, _ = x_prompt.shape
    DB, DS, _ = x_sample.shape
    n_cores = 8
    assert B * 2 == n_cores and DB == 4 * n_cores and DS == 64
    half = SEQ // 2
    NT = half // T
    assert NT * T == half
    if NT not in _PROG_CACHE:
        _PROG_CACHE[NT] = build_program(NT)
    nc = _PROG_CACHE[NT]

    shared = {
        "ident": np.eye(P, dtype=np.float32),
        "norm_g": f(norm_g).reshape(1, D),
        "w_in": f(w_in).reshape(D, DIN),
        "w_pool": f(w_pool).reshape(4, P, P),
        "pool_scale": f(pool_scale).reshape(1, DP),
        "ln_g": f(ln_g).reshape(1, DP),
        "ln_b": f(ln_b).reshape(1, DP),
        "w_s": f(w_s).reshape(4, P, P),
        "b_s": f(b_s).reshape(1, DP),
        "w_a": f(w_a).reshape(DP, D),
        "w_b": f(w_b).reshape(DP, D),
        "w_o": f(w_o).reshape(D, D),
        "final_g": f(final_g).reshape(1, D),
    }
    in_maps = []
    for c in range(n_cores):
        sq, hf = c // 2, c % 2
        m = dict(shared)
        m["xp"] = np.ascontiguousarray(x_prompt[sq, hf * half:(hf + 1) * half, :])
        if hf == 0:
            m["xh"] = np.zeros((H, D), np.float32)
        else:
            m["xh"] = np.ascontiguousarray(x_prompt[sq, half - H:half, :])
        m["xs"] = np.ascontiguousarray(x_sample[4 * c:4 * c + 4].reshape(T, D))
        m["cp"] = np.ascontiguousarray(cache_pool[0, 4 * c:4 * c + 4].reshape(60, DP))
        m["first"] = np.full((P, 1), 1.0 if hf == 0 else 0.0, np.float32)
        in_maps.append(m)

    res = run_bass_kernel_spmd(nc, in_maps, core_ids=list(range(n_cores)))
    outs = res.results

    y_prompt = np.empty((B, SEQ, D), np.float32)
    y_sample = np.empty((DB, DS, D), np.float32)
    st_pp = np.empty((1, B, 15, DP), np.float32)
    st_ps = np.empty((1, DB, 15, DP), np.float32)
    st_v = np.empty((1, DB, DS, DP), np.float32)
    for c in range(n_cores):
        sq, hf = c // 2, c % 2
        r = outs[c]
        y_prompt[sq, hf * half:(hf + 1) * half, :] = r["yp"]
        y_sample[4 * c:4 * c + 4] = r["ys"].reshape(4, DS, D)
        if hf == 1:
            st_pp[0, sq] = r["spp"]
        st_ps[0, 4 * c:4 * c + 4] = r["sps"].reshape(4, 15, DP)
        st_v[0, 4 * c:4 * c + 4] = r["sv"].reshape(4, DS, DP)
    return (y_prompt, y_sample, st_pp, st_ps, st_v)
```

```python
import numpy as np
from contextlib import ExitStack

import concourse.bass as bass
import concourse.mybir as mybir
from concourse.bass_utils import run_bass_kernel_spmd

F32 = mybir.dt.float32
BF16 = mybir.dt.bfloat16
ALU = mybir.AluOpType
AF = mybir.ActivationFunctionType

P = 128
D = 1024
DIN = 4608
DP = 512
T = 256
NSUB = 2
H = 16
EPS = 1e-6
WINDOWS = (2, 4, 8, 16)
N_PE_TR = 2
NPS = 8

C_A, C_GA, C_U, C_V, C_GB, C_RA, C_RB = 0, 512, 1024, 1536, 2048, 2560, 3584


class _Op:
    __slots__ = ("eng", "emit", "deps", "token", "is_dma", "name", "needed")


class Sched:
    ENGS = ("pe", "act", "dve", "pool", "sp")

    def __init__(self, nc, stack):
        self.nc = nc
        self.ops = {e: [] for e in self.ENGS}
        self.last_writer = {}
        self.readers = {}
        self.eng_sem = {e: stack.enter_context(nc.semaphore("prog_" + e)) for e in ("pe", "act", "dve", "pool")}
        self.stack = stack
        self.dma_sems = {}
        self.dma_cnt = {}

    def _dma_sem(self, name):
        if name not in self.dma_sems:
            self.dma_sems[name] = self.stack.enter_context(self.nc.semaphore("dma_" + name))
            self.dma_cnt[name] = 0
        return self.dma_sems[name]

    def add(self, eng, emit, reads=(), writes=(), dma=None, name=""):
        op = _Op()
        op.eng, op.emit, op.name = eng, emit, name
        op.is_dma = dma is not None
        op.needed = False
        op.token = None
        deps = []
        for k in reads:
            w = self.last_writer.get(k)
            if w is not None:
                deps.append((w, "raw"))
        for k in writes:
            w = self.last_writer.get(k)
            if w is not None:
                deps.append((w, "waw"))
            for r in self.readers.get(k, ()):
                deps.append((r, "war"))
        for k in reads:
            self.readers.setdefault(k, []).append(op)
        for k in writes:
            self.last_writer[k] = op
            self.readers[k] = []
        final = []
        seen = set()
        for d, kind in deps:
            if d is op or id(d) in seen:
                continue
            if (not op.is_dma) and (not d.is_dma) and d.eng == eng == "pe":
                continue
            seen.add(id(d))
            final.append(d)
            d.needed = True
        op.deps = final
        if op.is_dma:
            sem = self._dma_sem(dma)
            self.dma_cnt[dma] += 16
            op.token = (sem, self.dma_cnt[dma])
        self.ops[eng].append(op)
        return op

    def finalize(self):
        for e in ("pe", "act", "dve", "pool"):
            c = 0
            for op in self.ops[e]:
                if not op.is_dma and op.needed:
                    c += 1
                    op.token = (self.eng_sem[e], c)

    def emit_engine(self, eng_name, eng):
        seen = {}
        for op in self.ops[eng_name]:
            for d in op.deps:
                sem, val = d.token
                if seen.get(id(sem), 0) >= val:
                    continue
                seen[id(sem)] = val
                eng.wait_ge(sem, val)
            ins = op.emit(eng)
            if op.token is not None:
                if op.is_dma:
                    ins.then_inc(op.token[0], 16)
                else:
                    ins.then_inc(op.token[0], 1)


def build_program(NT):
    nc = bass.Bass("TRN2", target_bir_lowering=False)

    def din(name, shape):
        return nc.dram_tensor(name, list(shape), F32, kind="ExternalInput").ap()

    def dout(name, shape):
        return nc.dram_tensor(name, list(shape), F32, kind="ExternalOutput").ap()

    xp_d = din("xp", [NT * T, D])
    xh_d = din("xh", [H, D])
    xs_d = din("xs", [T, D])
    cp_d = din("cp", [60, DP])
    first_d = din("first", [P, 1])
    ident_d = din("ident", [P, P])
    ng_d = din("norm_g", [1, D])
    win_d = din("w_in", [D, DIN])
    wpool_d = din("w_pool", [4, P, P])
    psc_d = din("pool_scale", [1, DP])
    lng_d = din("ln_g", [1, DP])
    lnb_d = din("ln_b", [1, DP])
    ws_d = din("w_s", [4, P, P])
    bs_d = din("b_s", [1, DP])
    wa_d = din("w_a", [DP, D])
    wb_d = din("w_b", [DP, D])
    wo_d = din("w_o", [D, D])
    fg_d = din("final_g", [1, D])

    yp_d = dout("yp", [NT * T, D])
    ys_d = dout("ys", [T, D])
    spp_d = dout("spp", [15, DP])
    sps_d = dout("sps", [60, DP])
    sv_d = dout("sv", [T, DP])

    with ExitStack() as st:
        def sb(name, shape, dt):
            return st.enter_context(nc.sbuf_tensor(name, list(shape), dt))

        def pst(name, shape, dt):
            return st.enter_context(nc.psum_tensor(name, list(shape), dt))

        Win = sb("Win", [P, 8, DIN], BF16)
        Wa = sb("Wa", [P, 4, D], BF16)
        Wb = sb("Wb", [P, 4, D], BF16)
        Wo = sb("Wo", [P, 8, D], BF16)
        Wp = sb("Wp", [P, 4, P], BF16)
        WTp = sb("WTp", [P, 4, P], BF16)
        WTs = sb("WTs", [P, 4, P], BF16)
        g_row = sb("g_row", [P, D], F32)
        fg_row = sb("fg_row", [P, D], F32)
        B2p = sb("B2p", [P, 4, P], F32)
        B2s = sb("B2s", [P, 4, P], F32)
        identf = sb("identf", [P, P], F32)
        identb = sb("identb", [P, P], BF16)
        ones = sb("ones", [P, P], F32)
        cols = sb("cols", [P, 16], F32)
        invc = sb("invc", [P, 4, H], F32)
        dA = sb("dA", [P, 4, H], F32)
        stats = sb("stats", [P, 512], F32)
        xb = [sb("xb%d" % i, [P, NSUB, D], F32) for i in range(2)]
        xn = [sb("xn%d" % i, [P, D], BF16) for i in range(2)]
        hT = [sb("hT%d" % i, [P, 8, T], BF16) for i in range(2)]
        hTh = sb("hTh", [P, 8, H], BF16)
        a_buf = sb("a_buf", [P, 4, 320], F32)
        pA = sb("pA", [P, 4, 320], F32)
        pB = sb("pB", [P, 4, 320], F32)
        pa = sb("pa", [P, 4, T], BF16)
        sga = sb("sga", [P, 4, T], F32)
        fa = sb("fa", [P, 4, T], BF16)
        sra = sb("sra", [P, 8, T], F32)
        srb = sb("srb", [P, 8, T], F32)
        mg = sb("mg", [P, 8, T], BF16)
        n_buf = sb("n_buf", [P, NSUB, DP], BF16)
        sgb = sb("sgb", [P, 4, T], F32)
        s_bf = sb("s_bf", [P, 4, T], BF16)
        tmpc = sb("tmpc", [P, H], F32)
        cptb = sb("cptb", [64, 512], F32)

        ps = [pst("ps%d" % i, [P, 512], F32) for i in range(NPS)]

        S = Sched(nc, st)

        def v4(ap2):
            return ap2.rearrange("p a (b c) -> p (a b) c", c=P)
        Wnat_p, kWnp = v4(srb[:, 0:2, :]), ("srb", 0)
        Wnat_s, kWns = v4(srb[:, 2:4, :]), ("srb", 1)
        WT32p, kW32p = v4(srb[:, 4:6, :]), ("srb", 2)
        WT32s, kW32s = v4(srb[:, 6:8, :]), ("srb", 3)
        mgf = mg[:, :, :].rearrange("p a b -> p (a b)").bitcast(F32)
        bsr_p = mgf[:, 0:512].rearrange("p (g i) -> p g i", g=4)
        bsr_s = mgf[:, 512:1024].rearrange("p (g i) -> p g i", g=4)
        srb_flat = srb[:, :, :].rearrange("p a b -> p (a b)")
        sra_flat = sra[:, :, :].rearrange("p a b -> p (a b)")
        stp = sra_flat[0:H, 0:512]
        sga_flat = sga[:, :, :].rearrange("p a b -> p (a b)")
        pA_flat = pA[:, :, :].rearrange("p a b -> p (a b)")
        pB_flat = pB[:, :, :].rearrange("p a b -> p (a b)")
        n32 = pA_flat[:, 0:1024].rearrange("p (s c) -> p s c", s=NSUB)
        lng_row = pB_flat[:, 0:512]
        lnb_row = pB_flat[:, 512:1024]

        lng_col = lambda g: cols[:, g:g + 1]
        lnb_col = lambda g: cols[:, 4 + g:5 + g]
        psc_col = lambda g: cols[:, 8 + g:9 + g]
        first_col = cols[:, 12:13]
        neghalf = cols[:, 13:14]

        stat_ctr = [0]

        def stat(n=1):
            slot = stat_ctr[0] % 64
            stat_ctr[0] += 1
            return stats[:, slot * 8:slot * 8 + n], ("stat", slot)

        ps_ctr = [0]

        def next_ps():
            k = ps_ctr[0] % NPS
            ps_ctr[0] += 1
            return ps[k], ("ps", k)

        uniq = [0]

        def dma(eng, out, in_, reads, writes, sem, name=""):
            if sem[0] in "cw":
                uniq[0] += 1
                sem = "%s_%d" % (sem, uniq[0])
            return S.add(eng, lambda e, o=out, i=in_: e.dma_start(out=o, in_=i), reads=reads, writes=writes,
                         dma=sem, name=name)

        def x_rows(src, r0):
            return src[r0:r0 + T, :].rearrange("(s p) d -> p s d", p=P)

        tiles = [("p", i) for i in range(NT)] + [("s", 0)]
        NTT = len(tiles)

        def load_x(ti):
            kind, i = tiles[ti]
            b = ti % 2
            src = x_rows(xp_d, i * T) if kind == "p" else x_rows(xs_d, 0)
            return dma("sp", xb[b][:, :, :], src, [], [("xb", b, 0), ("xb", b, 1)], "x%d" % b, "loadx")

        cpt = cptb[:, :]
        S.add("dve", lambda e: e.memset(cpt, 0.0), [], ["cpt_m"])
        dma("sp", xb[1][0:H, 0, :], xh_d[:, :], [], [("xb", 1, 0)], "x1", "loadxh")
        dma("sp", g_row[:, :], ng_d.partition_broadcast(P), [], ["g_row"], "c")
        load_x(0)
        dma("sp", identf[:, :], ident_d[:, :], [], ["identf"], "c")
        dma("pool", identb[:, :], ident_d[:, :], [], ["identb"], "c")
        S.add("dve", lambda e: e.memset(cols[:, 13:14], -0.5), [], [("cols", 13)])
        S.add("dve", lambda e: e.memset(ones[:, :], 1.0), [], ["ones"])

        win_v = win_d.rearrange("(k p) n -> p k n", p=P)

        def load_win(cb0, cb1):
            dma("pool", Win[:, :, cb0 * 512:cb1 * 512], win_v[:, :, cb0 * 512:cb1 * 512], [],
                [("Win", cb) for cb in range(cb0, cb1)], "w", "win")
        load_win(0, 2)
        load_win(2, 5)
        dma("pool", Wp[:, :, :], wpool_d.rearrange("g c d -> c g d"), [], ["Wp"], "w", "wp")
        load_win(5, 7)
        S.add("dve", lambda e: e.memset(Wnat_s, 0.0), [], [kWns])
        ws_v = ws_d.rearrange("g i j -> i g j")
        dma("pool", Wnat_p, ws_v, [], [kWnp], "w")
        dma("pool", Wnat_s[64:128, :, 64:128], ws_v[0:64, :, 0:64], [], [kWns], "w")
        dma("pool", mgf[:, 0:512], bs_d.partition_broadcast(P), [], ["bsr_p"], "w")
        load_win(7, 9)
        dma("pool", Wa[:, :, :], wa_d.rearrange("(k p) n -> p k n", p=P), [], ["Wa"], "w", "wa")
        dma("pool", Wb[:, :, :], wb_d.rearrange("(k p) n -> p k n", p=P), [], ["Wb"], "w", "wb")
        dma("pool", Wo[:, :, :], wo_d.rearrange("(k p) n -> p k n", p=P), [], ["Wo"], "w", "wo")
        dma("pool", fg_row[:, :], fg_d.partition_broadcast(P), [], ["fg_row"], "w", "fg")

        def rsqrt_chain(src, pre_scale, reads, nparts=P):
            tt, ktt = stat()
            rs, krs = stat()
            S.add("dve", lambda e: e.tensor_scalar(out=tt[0:nparts, :], in0=src, scalar1=pre_scale, scalar2=EPS,
                                                   op0=ALU.mult, op1=ALU.add), reads, [ktt])
            S.add("pool", lambda e: e.tensor_tensor(out=rs[0:nparts, :], in0=tt[0:nparts, :],
                                                    in1=neghalf[0:nparts, :], op=ALU.pow),
                  [ktt, ("cols", 13)], [krs])
            return rs, krs

        def xprep_sub(xsrc, kx, s_idx, nparts=P):
            ss, kss = stat()
            kxn = ("xn", s_idx)
            S.add("act", lambda e: e.activation(out=xn[s_idx][0:nparts, :], in_=xsrc, func=AF.Square,
                                                accum_out=ss[0:nparts, :]), [kx], [kss, kxn])
            rs, krs = rsqrt_chain(ss[0:nparts, :], 1.0 / D, [kss], nparts)
            S.add("dve", lambda e: e.scalar_tensor_tensor(out=xn[s_idx][0:nparts, :], in0=xsrc,
                                                          scalar=rs[0:nparts, :], in1=g_row[0:nparts, :],
                                                          op0=ALU.mult, op1=ALU.mult),
                  [kx, krs, "g_row"], [kxn])

        def xprep_tile(ti):
            b = ti % 2
            for s in range(NSUB):
                xprep_sub(xb[b][:, s, :], ("xb", b, s), s)

        def bfview(bank):
            return bank[:, :].bitcast(BF16).rearrange("p (k t) -> p k t", k=8)

        def transposes_tile_pe(ti):
            b = ti % 2
            for s in range(NSUB):
                bank, bk = next_ps()
                ptv = bfview(bank)

                def tr(e, s=s, ptv=ptv):
                    ins = None
                    for kc in range(8):
                        ins = e.transpose(out=ptv[:, kc, :], in_=xn[s][:, kc * P:(kc + 1) * P],
                                          identity=identb[:, :])
                    return ins
                S.add("pe", tr, [("xn", s), "identb"], [bk])
                S.add("act", lambda e, s=s, b=b, ptv=ptv: e.activation(out=hT[b][:, :, s * P:(s + 1) * P],
                                                                        in_=ptv, func=AF.Copy),
                      [bk], [("hT", b, s)])

        def transposes_tile(ti):
            b = ti % 2
            for s in range(NSUB):
                S.add("sp", lambda e, s=s, b=b: e.dma_start_transpose(out=hT[b][:, :, s * P:(s + 1) * P],
                                                                      in_=xn[s][:, :]),
                      reads=[("xn", s)], writes=[("hT", b, s)], dma="tr%d%d" % (b, s), name="xpose")

        def proj_pb(col0, rhs_of_kc, nk, wkeys, lhs_of, rkeys):
            bank, bk = next_ps()

            def mm(e):
                ins = None
                for h in range(2):
                    for kc in range(nk):
                        ins = e.matmul(out=bank[:, h * T:(h + 1) * T], lhsT=lhs_of(kc, col0 + h * P),
                                       rhs=rhs_of_kc(kc), start=(kc == 0), stop=(kc == nk - 1))
                return ins
            S.add("pe", mm, list(wkeys) + list(rkeys), [bk])
            return bank, bk

        def win_pb(b, col0):
            return proj_pb(col0, lambda kc: hT[b][:, kc, :], 8, [("Win", col0 // 512)],
                           lambda kc, c: Win[:, kc, c:c + P], [("hT", b, 0), ("hT", b, 1)])

        def bank3(bank):
            return bank[:, :].rearrange("p (h t) -> p h t", h=2)

        def pv(buf, g0, g1, lo, hi, sample):
            if sample:
                return buf[:, g0:g1, :].rearrange("p g (s l) -> p g s l", s=4)[:, :, :, lo:hi]
            return buf[:, g0:g1, lo:hi]

        def pv1(buf, g, lo, hi, sample):
            if sample:
                return buf[:, g, :].rearrange("p (s l) -> p s l", s=4)[:, :, lo:hi]
            return buf[:, g, lo:hi]

        def pooling(sample, first_tile, carry=True):
            L = 80 if sample else H + T
            A, B, X = pA, pB, a_buf

            def add(dst, src, g0, g1, lo, k, rk, wk):
                S.add("dve", lambda e: e.tensor_tensor(out=pv(dst, g0, g1, lo, L, sample),
                                                       in0=pv(src, g0, g1, lo, L, sample),
                                                       in1=pv(src, g0, g1, lo - k, L - k, sample), op=ALU.add),
                      rk, wk)
            add(A, X, 0, 4, 1, 1, ["a_buf"], ["pA"])
            add(B, A, 1, 4, 3, 2, ["pA"], ["pB"])
            add(A, B, 2, 4, 7, 4, ["pB"], ["pA"])
            add(B, A, 3, 4, 15, 8, ["pA"], ["pB"])
            srcs = (A, B, A, B)
            for g, w in enumerate(WINDOWS):
                Sb = srcs[g]
                outv = pa[:, g, :].rearrange("p (s l) -> p s l", s=4) if sample else pa[:, g, :]
                S.add("dve", lambda e, Sb=Sb, g=g, w=w, outv=outv: e.scalar_tensor_tensor(
                    out=outv, in0=pv1(Sb, g, H, L, sample), scalar=1.0 / w, in1=pv1(X, g, H, L, sample),
                    op0=ALU.mult, op1=ALU.subtract), ["pA", "pB", "a_buf"], [("pa", g)])
                if first_tile:
                    S.add("dve", lambda e, Sb=Sb, g=g: e.tensor_tensor(
                        out=tmpc[:, :], in0=Sb[:, g, H:2 * H], in1=invc[:, g, :], op=ALU.mult),
                        ["pA", "pB", "invc"], ["tmpc"])
                    S.add("dve", lambda e, g=g: e.tensor_tensor(
                        out=pa[:, g, 0:H], in0=tmpc[:, :], in1=X[:, g, H:2 * H], op=ALU.subtract),
                        ["tmpc", "a_buf"], [("pa", g)])
            if (not sample) and carry:
                S.add("dve", lambda e: e.tensor_copy(out=X[:, :, 0:H], in_=X[:, :, T:T + H]),
                      ["a_buf", "pA", "pB"], ["a_buf"])

        st_ctr = [0]

        def state_out(src_of_g, dst_rows, sem):
            bank, bk = next_ps()

            def tr(e):
                ins = None
                for g in range(4):
                    ins = e.transpose(out=bank[0:H, g * P:(g + 1) * P], in_=src_of_g(g), identity=identf[:, :])
                return ins
            S.add("pe", tr, ["a_buf", "identf"], [bk])
            k = st_ctr[0] % 4
            st_ctr[0] += 1
            S.add("act", lambda e: e.activation(out=srb_flat[0:H, 512 * k:512 * (k + 1)], in_=bank[0:H, :],
                                                func=AF.Copy), [bk], [("srb", k)])
            return dma("sp", dst_rows, srb_flat[1:H, 512 * k:512 * (k + 1)], [("srb", k)], [], "%s%d" % (sem, k),
                       "state")

        out_ops = []

        xprep_sub(xb[1][0:H, 0, :], ("xb", 1, 0), 1, nparts=H)

        bankt, bkt = next_ps()
        ptvh = bfview(bankt)

        def trh(e):
            ins = None
            for kc in range(8):
                ins = e.transpose(out=ptvh[:, kc, 0:H], in_=xn[1][0:H, kc * P:(kc + 1) * P],
                                  identity=identb[0:H, 0:H])
            return ins
        S.add("pe", trh, [("xn", 1), "identb"], [bkt])
        S.add("act", lambda e: e.activation(out=hTh[:, :, :], in_=ptvh[:, :, 0:H], func=AF.Copy),
              [bkt], ["hTh"])
        xprep_tile(0)
        transposes_tile_pe(0)
        bankh, bkh = next_ps()

        def mmh(e):
            ins = None
            for oc in range(4):
                for kc in range(8):
                    ins = e.matmul(out=bankh[:, oc * H:(oc + 1) * H], lhsT=Win[:, kc, oc * P:(oc + 1) * P],
                                   rhs=hTh[:, kc, :], start=(kc == 0), stop=(kc == 7))
            return ins
        S.add("pe", mmh, [("Win", 0), "hTh"], [bkh])
        S.add("act", lambda e: e.activation(out=a_buf[:, :, 0:H],
                                            in_=bankh[:, 0:4 * H].rearrange("p (g h) -> p g h", g=4),
                                            func=AF.Copy), [bkh], ["a_buf"])

        rows12 = sra_flat[0:12, 512:640]
        krows = [("sra", 1), "rows_a", "rows_b", "rows_c"]
        for vi, (vec_d, kk) in enumerate(((lng_d, "rows_a"), (lnb_d, "rows_b"), (psc_d, "rows_c"))):
            dma("sp", rows12[4 * vi:4 * vi + 4, :], vec_d[0:1, :].rearrange("o (g p) -> (o g) p", g=4),
                [], [kk], "c")
        dma("sp", cols[:, 12:13], first_d[:, :], [], [("cols", 12)], "c")
        def make_cols():
            bankr, bkr = next_ps()

            def mmc(e):
                return e.transpose(out=bankr[:, 0:12], in_=rows12, identity=identf[0:12, 0:12])
            S.add("pe", mmc, ["identf"] + krows, [bkr])
            S.add("act", lambda e: e.activation(out=cols[:, 0:12], in_=bankr[:, 0:12], func=AF.Copy),
                  [bkr], [("cols", j) for j in range(12)])

        def late_consts(ti):
            if ti == min(2, NTT - 2):
                for sg in range(4):
                    dma("sp", cptb[sg * H + 1:(sg + 1) * H, :], cp_d[sg * 15:(sg + 1) * 15, :], ["cpt_m"],
                        [("cpt", sg)], "c", "cp")

        kbss_all = ["bsr_s"]

        def setup_spatial_pre():
            S.add("dve", lambda e: e.tensor_copy(out=Wnat_s[0:64, :, 0:64], in_=Wnat_p[0:64, :, 0:64]),
                  [kWnp], [kWns])
            S.add("dve", lambda e: e.memset(Wnat_p[0:64, :, 64:128], 0.0), [], [kWnp])
            for hh in range(2):
                S.add("dve", lambda e, hh=hh: e.tensor_copy(out=bsr_s[:, :, hh * 64:(hh + 1) * 64],
                                                             in_=bsr_p[:, :, 0:64]), ["bsr_p"], ["bsr_s"])

        def setup_spatial():
          setup_spatial_pre()
          modes = (("p", Wnat_p, kWnp, WT32p, kW32p, WTp, bsr_p, ["bsr_p"], B2p),
                   ("s", Wnat_s, kWns, WT32s, kW32s, WTs, bsr_s, kbss_all, B2s))
          for (tag, Wnat, kWn, WT32, kW32, WTb, bsr, kbs, B2) in modes:
            bank, bk = next_ps()

            def tr(e, Wnat=Wnat, bank=bank):
                ins = None
                for g in range(4):
                    ins = e.transpose(out=bank[:, g * P:(g + 1) * P], in_=Wnat[:, g, :], identity=identf[:, :])
                return ins
            S.add("pe", tr, [kWn, "identf"], [bk])
            S.add("act", lambda e, WT32=WT32, bank=bank: e.activation(
                out=WT32, in_=bank[:, :].rearrange("p (g i) -> p g i", g=4), func=AF.Copy), [bk], [kW32])
            S.add("dve", lambda e, WTb=WTb, WT32=WT32: e.tensor_copy(out=WTb[:, :, :], in_=WT32),
                  [kW32], [("WT", tag)])
          for (tag, Wnat, kWn, WT32, kW32, WTb, bsr, kbs, B2) in modes:
            bank2, bk2 = next_ps()

            def rw(e, WT32=WT32, bank2=bank2):
                ins = None
                for g in range(4):
                    ins = e.matmul(out=bank2[:, g * P:(g + 1) * P], lhsT=ones[:, :], rhs=WT32[:, g, :],
                                   start=True, stop=True)
                return ins
            S.add("pe", rw, [kW32, "ones"], [bk2])
            for g in range(4):
                S.add("dve", lambda e, g=g, B2=B2, bank2=bank2, bsr=bsr: e.scalar_tensor_tensor(
                    out=B2[:, g, :], in0=bank2[:, g * P:(g + 1) * P], scalar=lnb_col(g), in1=bsr[:, g, :],
                    op0=ALU.mult, op1=ALU.add), [bk2, ("cols", 4 + g)] + [("mg", q) for q in range(4)] + list(kbs),
                    [("B2", tag)])

        S.add("dve", lambda e: e.memset(dA[:, :, :], 0.0), [], ["dA"])
        for g, w in enumerate(WINDOWS):
            for t in range(w - 1):
                S.add("dve", lambda e, g=g, t=t, w=w: e.memset(dA[:, g, t:t + 1], 1.0 / (t + 1) - 1.0 / w),
                      [], ["dA"])
        for g, w in enumerate(WINDOWS):
            S.add("dve", lambda e, g=g, w=w: e.tensor_scalar(
                out=invc[:, g, :], in0=dA[:, g, :], scalar1=first_col, scalar2=1.0 / w,
                op0=ALU.mult, op1=ALU.add), ["dA", ("cols", 12)], ["invc"])

        def stage_A(ti):
            kind, i = tiles[ti]
            sample = kind == "s"
            b = ti % 2
            tag = "s" if sample else "p"
            B2 = B2s if sample else B2p
            WT = WTs if sample else WTp

            if sample:
                bankc, bkc = next_ps()

                def trc(e, bankc=bankc):
                    ins = None
                    for g in range(4):
                        ins = e.transpose(out=bankc[:, g * 64:(g + 1) * 64], in_=cpt[:, g * P:(g + 1) * P],
                                          identity=identf[0:64, 0:64])
                    return ins
                S.add("pe", trc, ["cpt_m"] + [("cpt", sg) for sg in range(4)] + ["identf"], [bkc])
                S.add("act", lambda e, bankc=bankc: e.activation(
                    out=a_buf[:, :, :].rearrange("p g (s l) -> p g s l", s=4)[:, :, :, 0:H],
                    in_=bankc[:, 0:256].rearrange("p (g s h) -> p g s h", g=4, s=4), func=AF.Copy),
                    [bkc], ["a_buf"])

            for pb in range(2):
                bank, bk = win_pb(b, C_A + pb * 256)
                if sample:
                    outv = a_buf[:, 2 * pb:2 * pb + 2, :].rearrange("p g (s l) -> p g s l", s=4)[:, :, :, H:80]
                    inv = bank[:, :].rearrange("p (g s l) -> p g s l", g=2, s=4)
                else:
                    outv = a_buf[:, 2 * pb:2 * pb + 2, H:H + T]
                    inv = bank3(bank)
                S.add("act", lambda e, outv=outv, inv=inv: e.activation(out=outv, in_=inv, func=AF.Copy),
                      [bk], ["a_buf"])
            for pb in range(2):
                bank, bk = win_pb(b, C_GA + pb * 256)
                S.add("act", lambda e, pb=pb, bank=bank: e.activation(
                    out=sga[:, 2 * pb:2 * pb + 2, :], in_=bank3(bank), func=AF.Silu), [bk], [("sga", pb)])
            pooling(sample, first_tile=(ti == 0), carry=not ((not sample) and i == NT - 1))
            if sample:
                dma("sp", lng_row, lng_d.partition_broadcast(P), [], ["pB"], "c")
                dma("sp", lnb_row, lnb_d.partition_broadcast(P), [], ["pB"], "c")

        def state_outs(ti):
            kind, i = tiles[ti]
            if kind == "p" and i == NT - 1:
                out_ops.append(state_out(lambda g: a_buf[:, g, T:T + H], spp_d[:, :], "st"))
            if kind == "s":
                for sg in range(4):
                    out_ops.append(state_out(
                        lambda g, sg=sg: a_buf[:, g, sg * 80 + 64:sg * 80 + 80],
                        sps_d[sg * 15:(sg + 1) * 15, :], "st"))


        def stage_B(ti):
            kind, i = tiles[ti]
            sample = kind == "s"
            b = ti % 2
            tag = "s" if sample else "p"
            B2 = B2s if sample else B2p
            WT = WTs if sample else WTp

            if ti + 1 < NTT:
                load_x(ti + 1)
            late_consts(ti)
            if ti == 0:
                make_cols()

            ln_defer = []
            for s in range(NSUB):
                bank, bk = next_ps()

                def mmv(e, s=s, bank=bank):
                    ins = None
                    for kc in range(8):
                        ins = e.matmul(out=bank[:, :], lhsT=hT[b][:, kc, s * P:(s + 1) * P],
                                       rhs=Win[:, kc, C_V:C_V + 512], start=(kc == 0), stop=(kc == 7))
                    return ins
                S.add("pe", mmv, [("Win", 3), ("hT", b, s)], [bk])
                st6, k6 = stat(6)
                mv, kmv = stat(2)
                nmr, knm = stat()
                S.add("dve", lambda e, st6=st6, bank=bank: e.bn_stats(out=st6, in_=bank[:, :]), [bk], [k6])
                S.add("dve", lambda e, st6=st6, mv=mv: e.bn_aggr(out=mv, in_=st6), [k6], [kmv])
                rstd, krs = rsqrt_chain(mv[:, 1:2], 1.0, [kmv])
                S.add("dve", lambda e, mv=mv, rstd=rstd, nmr=nmr: e.tensor_scalar(
                    out=nmr, in0=mv[:, 0:1], scalar1=rstd, scalar2=-1.0, op0=ALU.mult, op1=ALU.mult),
                    [kmv, krs], [knm])
                ln_defer.append((s, bank, bk, rstd, krs, nmr, knm))

            def ln_finish():
              for (s, bank, bk, rstd, krs, nmr, knm) in ln_defer:
                S.add("act", lambda e, s=s, bank=bank, rstd=rstd, nmr=nmr: e.activation(
                    out=n_buf[:, s, :], in_=bank[:, :], func=AF.Identity, scale=rstd, bias=nmr),
                    [bk, krs, knm], [("n", s)])
                if sample:
                    S.add("act", lambda e, s=s, bank=bank, rstd=rstd, nmr=nmr: e.activation(
                        out=n32[:, s, :], in_=bank[:, :], func=AF.Identity, scale=rstd, bias=nmr),
                        [bk, krs, knm], ["pA"])
                    S.add("dve", lambda e, s=s: e.tensor_tensor(out=n32[:, s, :], in0=n32[:, s, :], in1=lng_row,
                                                                op=ALU.mult), ["pA", "pB"], ["pA"])
                    S.add("dve", lambda e, s=s: e.tensor_tensor(out=n32[:, s, :], in0=n32[:, s, :], in1=lnb_row,
                                                                op=ALU.add), ["pA", "pB"], ["pA"])
                    out_ops.append(dma("sp", sv_d[s * P:(s + 1) * P, :], n32[:, s, :], ["pA"], [], "sv%d" % s))

            for pb in range(2):
                bank, bk = win_pb(b, C_GB + pb * 256)
                S.add("act", lambda e, pb=pb, bank=bank: e.activation(
                    out=sgb[:, 2 * pb:2 * pb + 2, :], in_=bank3(bank), func=AF.Silu), [bk], [("sgb", pb)])
            for pb in range(2):
                bank, bk = win_pb(b, C_U + pb * 256)
                S.add("dve", lambda e, pb=pb, bank=bank: e.tensor_tensor(
                    out=sgb[:, 2 * pb:2 * pb + 2, :], in0=bank3(bank), in1=sgb[:, 2 * pb:2 * pb + 2, :],
                    op=ALU.mult), [bk, ("sgb", pb)], [("sgb", pb)])

            ln_finish()
            if ti + 1 < NTT:
                xprep_tile(ti + 1)
                if ti + 1 >= N_PE_TR:
                    transposes_tile(ti + 1)

            for pb in range(4):
                bank, bk = win_pb(b, C_RA + pb * 256)
                S.add("act", lambda e, pb=pb, bank=bank: e.activation(
                    out=sra[:, 2 * pb:2 * pb + 2, :], in_=bank3(bank), func=AF.Sigmoid),
                    [bk], [("sra", pb)])

            if ti == 0:
                setup_spatial()
            state_outs(ti)

            for pb in range(2):
                bank, bk = next_ps()

                def mmp(e, pb=pb, bank=bank):
                    ins = None
                    for h in range(2):
                        g = 2 * pb + h
                        ins = e.matmul(out=bank[:, h * T:(h + 1) * T], lhsT=Wp[:, g, :], rhs=pa[:, g, :],
                                       start=True, stop=True)
                    return ins
                S.add("pe", mmp, ["Wp", ("pa", 2 * pb), ("pa", 2 * pb + 1)], [bk])
                for h in range(2):
                    g = 2 * pb + h
                    S.add("dve", lambda e, g=g, h=h, bank=bank: e.scalar_tensor_tensor(
                        out=fa[:, g, :], in0=bank[:, h * T:(h + 1) * T], scalar=psc_col(g), in1=sga[:, g, :],
                        op0=ALU.mult, op1=ALU.mult), [bk, ("cols", 8 + g), ("sga", pb)], [("fa", g)])

            for pb in range(2):
                bank, bk = next_ps()

                def mms(e, pb=pb, bank=bank):
                    ins = None
                    for h in range(2):
                        g = 2 * pb + h
                        for s in range(NSUB):
                            ins = e.matmul(out=bank[:, h * T + s * P:h * T + (s + 1) * P],
                                           lhsT=n_buf[:, s, g * P:(g + 1) * P], rhs=WT[:, g, :],
                                           start=True, stop=True)
                    return ins
                S.add("pe", mms, [("WT", tag), ("n", 0), ("n", 1)], [bk])
                for h in range(2):
                    g = 2 * pb + h
                    for s in range(NSUB):
                        S.add("dve", lambda e, g=g, h=h, s=s, bank=bank: e.scalar_tensor_tensor(
                            out=sga[:, g, s * P:(s + 1) * P], in0=bank[:, h * T + s * P:h * T + (s + 1) * P],
                            scalar=lng_col(g), in1=B2[:, g, :], op0=ALU.mult, op1=ALU.add),
                            [bk, ("cols", g), ("B2", tag)], [("sga", pb)])
                S.add("dve", lambda e, pb=pb: e.tensor_tensor(
                    out=s_bf[:, 2 * pb:2 * pb + 2, :], in0=sga[:, 2 * pb:2 * pb + 2, :],
                    in1=sgb[:, 2 * pb:2 * pb + 2, :], op=ALU.mult), [("sga", pb), ("sgb", pb)], [("s", pb)])

            for pb in range(4):
                bank, bk = win_pb(b, C_RB + pb * 256)
                S.add("act", lambda e, pb=pb, bank=bank: e.activation(
                    out=srb[:, 2 * pb:2 * pb + 2, :], in_=bank3(bank), func=AF.Sigmoid),
                    [bk], [("srb", pb)])

            for pb in range(4):
                bank, bk = proj_pb(pb * 256, lambda kc: fa[:, kc, :], 4, ["Wa"],
                                   lambda kc, c: Wa[:, kc, c:c + P], [("fa", g) for g in range(4)])
                S.add("dve", lambda e, pb=pb, bank=bank: e.tensor_tensor(
                    out=sra[:, 2 * pb:2 * pb + 2, :], in0=bank3(bank), in1=sra[:, 2 * pb:2 * pb + 2, :],
                    op=ALU.mult), [bk, ("sra", pb)], [("sra", pb)])

            if ti + 1 < NTT and ti + 1 < N_PE_TR:
                transposes_tile_pe(ti + 1)

            for pb in range(4):
                bank, bk = proj_pb(pb * 256, lambda kc: s_bf[:, kc, :], 4, ["Wb"],
                                   lambda kc, c: Wb[:, kc, c:c + P], [("s", 0), ("s", 1)])
                S.add("dve", lambda e, pb=pb, bank=bank: e.tensor_tensor(
                    out=srb[:, 2 * pb:2 * pb + 2, :], in0=bank3(bank), in1=srb[:, 2 * pb:2 * pb + 2, :],
                    op=ALU.mult), [bk, ("srb", pb)], [("srb", pb)])
                if pb >= 1:
                    q = pb - 1
                    S.add("dve", lambda e, q=q: e.tensor_tensor(
                        out=mg[:, 2 * q:2 * q + 2, :], in0=sra[:, 2 * q:2 * q + 2, :],
                        in1=srb[:, 2 * q:2 * q + 2, :], op=ALU.add), [("sra", q), ("srb", q)], [("mg", q)])
            S.add("dve", lambda e: e.tensor_tensor(out=mg[:, 6:8, :], in0=sra[:, 6:8, :], in1=srb[:, 6:8, :],
                                                   op=ALU.add), [("sra", 3), ("srb", 3)], [("mg", 3)])


        def stage_C(ti):
            kind, i = tiles[ti]
            sample = kind == "s"
            b = ti % 2
            tag = "s" if sample else "p"
            B2 = B2s if sample else B2p
            WT = WTs if sample else WTp

            ydst = yp_d if not sample else ys_d
            r0 = i * T if not sample else 0
            last = ti == NTT - 1

            def wo_group(s, h):
                bank, bk = next_ps()
                if not last:
                    def mmo(e):
                        ins = None
                        for kc in range(8):
                            ins = e.matmul(out=bank[:, :], lhsT=mg[:, kc, s * P:(s + 1) * P],
                                           rhs=Wo[:, kc, h * 512:(h + 1) * 512], start=(kc == 0), stop=(kc == 7))
                        return ins
                    S.add("pe", mmo, ["Wo"] + [("mg", q) for q in range(4)], [bk])
                return bank, bk

            def resid(s, h, bank, bk):
                S.add("dve", lambda e: e.tensor_tensor(
                    out=xb[b][:, s, h * 512:(h + 1) * 512], in0=bank[:, :],
                    in1=xb[b][:, s, h * 512:(h + 1) * 512], op=ALU.add), [bk, ("xb", b, s)], [("xb", b, s)])

            def final_norm(s):
                ss2, kss2 = stat()
                S.add("act", lambda e: e.activation(out=xn[s][:, :], in_=xb[b][:, s, :], func=AF.Square,
                                                    accum_out=ss2), [("xb", b, s)], [kss2, ("xn", s)])
                rs2, krs2 = rsqrt_chain(ss2, 1.0 / D, [kss2])
                S.add("dve", lambda e: e.scalar_tensor_tensor(
                    out=xb[b][:, s, :], in0=xb[b][:, s, :], scalar=rs2, in1=fg_row[:, :],
                    op0=ALU.mult, op1=ALU.mult), [("xb", b, s), krs2, "fg_row"], [("xb", b, s)])
                out_ops.append(dma("sp", ydst[r0 + s * P:r0 + (s + 1) * P, :], xb[b][:, s, :],
                                   [("xb", b, s)], [], "y%d%d" % (b, s), "store"))

            if not last:
                for s in range(NSUB):
                    for h in range(2):
                        bank, bk = wo_group(s, h)
                        resid(s, h, bank, bk)
                for s in range(NSUB):
                    final_norm(s)
            else:
                groups = [(s, h) + wo_group(s, h) for s in range(NSUB) for h in range(2)]
                for q in range(4):
                    for (s, h, bank, bk) in groups:
                        def mmq(e, s=s, h=h, bank=bank, q=q):
                            ins = None
                            for kc in (2 * q, 2 * q + 1):
                                ins = e.matmul(out=bank[:, :], lhsT=mg[:, kc, s * P:(s + 1) * P],
                                               rhs=Wo[:, kc, h * 512:(h + 1) * 512], start=(kc == 0),
                                               stop=(kc == 7))
                            return ins
                        S.add("pe", mmq, ["Wo", ("mg", q)], [bk])
                for (s, h, bank, bk) in groups:
                    resid(s, h, bank, bk)
                for s in range(NSUB):
                    final_norm(s)


        stage_A(0)
        for ti in range(NTT):
            stage_B(ti)
            if ti + 1 < NTT:
                stage_A(ti + 1)
            stage_C(ti)

        fin = _Op()
        fin.eng, fin.name, fin.is_dma, fin.needed, fin.token = "sp", "final", False, False, None
        fin.deps = list(out_ops)
        fin.emit = lambda e: e.nop()
        S.ops["sp"].append(fin)

        S.finalize()
        with nc.Block() as block:
            @block.sync
            def _(eng):
                S.emit_engine("sp", eng)

            @block.tensor
            def _(eng):
                S.emit_engine("pe", eng)

            @block.scalar
            def _(eng):
                S.emit_engine("act", eng)

            @block.vector
            def _(eng):
                S.emit_engine("dve", eng)

            @block.gpsimd
            def _(eng):
                S.emit_engine("pool", eng)
    return nc


_PROG_CACHE = {}


def kernel(x_prompt, x_sample, cache_pool, norm_g, w_in, w_pool, pool_scale, ln_g, ln_b, w_s, b_s,
           w_a, w_b, w_o, final_g):
    f = lambda a: np.ascontiguousarray(np.asarray(a, dtype=np.float32))
    x_prompt, x_sample, cache_pool = f(x_prompt), f(x_sample), f(cache_pool)
    B, SEQ, _ = x_prompt.shape
    DB, DS, _ = x_sample.shape
    n_cores = 8
    assert B * 2 == n_cores and DB == 4 * n_cores and DS == 64
    half = SEQ // 2
    NT = half // T
    assert NT * T == half
    if NT not in _PROG_CACHE:
        _PROG_CACHE[NT] = build_program(NT)
    nc = _PROG_CACHE[NT]

    shared = {
        "ident": np.eye(P, dtype=np.float32),
        "norm_g": f(norm_g).reshape(1, D),
        "w_in": f(w_in).reshape(D, DIN),
        "w_pool": f(w_pool).reshape(4, P, P),
        "pool_scale": f(pool_scale).reshape(1, DP),
        "ln_g": f(ln_g).reshape(1, DP),
        "ln_b": f(ln_b).reshape(1, DP),
        "w_s": f(w_s).reshape(4, P, P),
        "b_s": f(b_s).reshape(1, DP),
        "w_a": f(w_a).reshape(DP, D),
        "w_b": f(w_b).reshape(DP, D),
        "w_o": f(w_o).reshape(D, D),
        "final_g": f(final_g).reshape(1, D),
    }
    in_maps = []
    for c in range(n_cores):
        sq, hf = c // 2, c % 2
        m = dict(shared)
        m["xp"] = np.ascontiguousarray(x_prompt[sq, hf * half:(hf + 1) * half, :])
        if hf == 0:
            m["xh"] = np.zeros((H, D), np.float32)
        else:
            m["xh"] = np.ascontiguousarray(x_prompt[sq, half - H:half, :])
        m["xs"] = np.ascontiguousarray(x_sample[4 * c:4 * c + 4].reshape(T, D))
        m["cp"] = np.ascontiguousarray(cache_pool[0, 4 * c:4 * c + 4].reshape(60, DP))
        m["first"] = np.full((P, 1), 1.0 if hf == 0 else 0.0, np.float32)
        in_maps.append(m)

    res = run_bass_kernel_spmd(nc, in_maps, core_ids=list(range(n_cores)))
    outs = res.results

    y_prompt = np.empty((B, SEQ, D), np.float32)
    y_sample = np.empty((DB, DS, D), np.float32)
    st_pp = np.empty((1, B, 15, DP), np.float32)
    st_ps = np.empty((1, DB, 15, DP), np.float32)
    st_v = np.empty((1, DB, DS, DP), np.float32)
    for c in range(n_cores):
        sq, hf = c // 2, c % 2
        r = outs[c]
        y_prompt[sq, hf * half:(hf + 1) * half, :] = r["yp"]
        y_sample[4 * c:4 * c + 4] = r["ys"].reshape(4, DS, D)
        if hf == 1:
            st_pp[0, sq] = r["spp"]
        st_ps[0, 4 * c:4 * c + 4] = r["sps"].reshape(4, 15, DP)
        st_v[0, 4 * c:4 * c + 4] = r["sv"].reshape(4, DS, DP)
    return (y_prompt, y_sample, st_pp, st_ps, st_v)
```
